# Optimizing a Trainium2 kernel written in Bass

```python
import jax, jax.numpy as jnp
from jax import lax
import numpy as np

D_MODEL = 1024
BATCH = 16
SEQ = 2048
DEPTH = 4

HEAD_DIM = 64
ROPE_THETA = 10000.0
A_HEADS = 8
A_CONFIGS = ((128, 1), (512, 4), (2048, 16))
B_HEADS = 8
B_KV_HEADS = 2
B_WINDOW = 128
C_HEADS = 8
IDX_HEADS = 8
IDX_DIM = 64
TOPK_MAX = 256
QUERY_BLOCK = 128
BAND_BLOCK = 128
D_FF = 2816
LN_EPS = 1e-5
DN_ALPHA = (2 * DEPTH) ** 0.25
DN_BETA = (8 * DEPTH) ** -0.25

A_W = A_HEADS * HEAD_DIM
B_QW = B_HEADS * HEAD_DIM
B_KW = B_KV_HEADS * HEAD_DIM
C_W = C_HEADS * HEAD_DIM
IN_SIZES = (A_W, A_W, A_W,
            B_QW, B_KW, B_KW,
            C_W, HEAD_DIM, HEAD_DIM,
            IDX_HEADS * IDX_DIM, IDX_DIM, IDX_HEADS,
            D_MODEL, D_MODEL, D_MODEL)
D_IN = sum(IN_SIZES)

kernel_name = "hybrid_dilated_swa_dsa_macaron_deepnorm"


def layer_norm(x, g, b):
    xf = x.astype(jnp.float32)
    mu = xf.mean(-1, keepdims=True)
    var = jnp.square(xf - mu).mean(-1, keepdims=True)
    y = (xf - mu) * lax.rsqrt(var + LN_EPS)
    return (y * g.astype(jnp.float32) + b.astype(jnp.float32)).astype(x.dtype)


def rope_tables(positions, dim):
    inv = ROPE_THETA ** (-jnp.arange(0, dim, 2, dtype=jnp.float32) / dim)
    ang = positions.astype(jnp.float32)[..., None] * inv
    return jnp.cos(ang), jnp.sin(ang)


def apply_rope(t, cos, sin):
    tf = t.astype(jnp.float32)
    t1, t2 = jnp.split(tf, 2, axis=-1)
    c = cos[:, :, None, :]
    s = sin[:, :, None, :]
    return jnp.concatenate([t1 * c - t2 * s, t2 * c + t1 * s], axis=-1).astype(t.dtype)


def swiglu(x, w_in, w_out):
    gate, up = jnp.split(x @ w_in, 2, axis=-1)
    return (jax.nn.silu(gate) * up) @ w_out


def banded_attention(q, k, v, max_dist, sink=None):
    n, l, h, dh = q.shape
    g = k.shape[2]
    r = h // g
    blk = BAND_BLOCK
    nb = -(-l // blk)
    pad = nb * blk - l
    qp = jnp.pad(q, ((0, 0), (0, pad), (0, 0), (0, 0))).reshape(n, nb, blk, g, r, dh)

    def kv_blocks(t):
        tp = jnp.pad(t, ((0, 0), (blk, pad), (0, 0), (0, 0))).reshape(n, nb + 1, blk, g, dh)
        return jnp.concatenate([tp[:, :-1], tp[:, 1:]], axis=2)

    kb, vb = kv_blocks(k), kv_blocks(v)
    s = jnp.einsum('nbqgrd,nbkgd->nbgrqk', qp, kb, preferred_element_type=jnp.float32) * (dh ** -0.5)
    qi = jnp.arange(blk)[:, None]
    kj = jnp.arange(2 * blk)[None, :]
    dist = qi + blk - kj
    kpos = jnp.arange(nb)[:, None, None] * blk - blk + kj[None]
    valid = (dist >= 0) & (dist <= max_dist) & (kpos >= 0)
    s = jnp.where(valid[:, None, None], s, -jnp.inf)
    m = s.max(-1)
    if sink is not None:
        sk = sink.astype(jnp.float32).reshape(g, r)[:, :, None]
        m = jnp.maximum(m, sk)
    p = jnp.exp(s - m[..., None])
    den = p.sum(-1)
    if sink is not None:
        den = den + jnp.exp(sk - m)
    o = jnp.einsum('nbgrqk,nbkgd->nbqgrd', p.astype(v.dtype), vb, preferred_element_type=jnp.float32)
    o = o / den.transpose(0, 1, 4, 2, 3)[..., None]
    o = o.reshape(n, nb * blk, h, dh)[:, :l].astype(q.dtype)
    lse = (m + jnp.log(den)).transpose(0, 1, 4, 2, 3).reshape(n, nb * blk, h)[:, :l]
    return o, lse


def dilated_attention(q, k, v):
    b, s, h, dh = q.shape
    outs, lses = [], []
    for window, dil in A_CONFIGS:
        def fold(t):
            return t.reshape(b, s // dil, dil, h, dh).transpose(0, 2, 1, 3, 4).reshape(b * dil, s // dil, h, dh)
        o, lse = banded_attention(fold(q), fold(k), fold(v), window // dil)
        outs.append(o.reshape(b, dil, s // dil, h, dh).transpose(0, 2, 1, 3, 4).reshape(b, s, h, dh))
        lses.append(lse.reshape(b, dil, s // dil, h).transpose(0, 2, 1, 3).reshape(b, s, h))
    wts = jax.nn.softmax(jnp.stack(lses, 0), axis=0)
    o = jnp.einsum('cbsh,cbshd->bshd', wts, jnp.stack(outs, 0).astype(jnp.float32))
    return o.astype(q.dtype)


def dsa_attention(q, k, v, q_idx, k_idx, w_idx):
    b, s, h, dh = q.shape
    k_sel = min(TOPK_MAX, s // 4)
    nq = s // QUERY_BLOCK
    key_pos = jnp.arange(s)
    gather = jax.vmap(lambda t, ix: t[ix])

    def block(i):
        start = i * QUERY_BLOCK
        qb = lax.dynamic_slice_in_dim(q, start, QUERY_BLOCK, axis=1)
        qib = lax.dynamic_slice_in_dim(q_idx, start, QUERY_BLOCK, axis=1)
        wb = lax.dynamic_slice_in_dim(w_idx, start, QUERY_BLOCK, axis=1)
        qpos = start + jnp.arange(QUERY_BLOCK)
        causal = key_pos[None, :] <= qpos[:, None]
        dots = jnp.einsum('bqhd,bsd->bqhs', qib, k_idx, preferred_element_type=jnp.float32)
        score = jnp.einsum('bqhs,bqh->bqs', jax.nn.relu(dots), wb.astype(jnp.float32))
        score = jnp.where(causal[None], score, -jnp.inf)
        _, idx = lax.top_k(score, k_sel)
        sel_valid = idx <= qpos[None, :, None]
        kg = gather(k, idx)
        vg = gather(v, idx)
        att = jnp.einsum('bqhd,bqkd->bhqk', qb, kg, preferred_element_type=jnp.float32) * (dh ** -0.5)
        att = jnp.where(sel_valid[:, None], att, -jnp.inf)
        p = jax.nn.softmax(att, axis=-1)
        return jnp.einsum('bhqk,bqkd->bqhd', p.astype(v.dtype), vg,
                          preferred_element_type=jnp.float32).astype(q.dtype)

    out = lax.map(block, jnp.arange(nq))
    return out.transpose(1, 0, 2, 3, 4).reshape(b, s, h, dh)


def hybrid_mixer(x, cos, sin, w_in, sink_b, w_br_a, w_br_b, w_br_c, w_out):
    b, s, _ = x.shape
    proj = x @ w_in
    splits = [int(c) for c in np.cumsum(IN_SIZES)[:-1]]
    (qa, ka, va, qb, kb, vb, qc, kc, vc, qi, ki, wi, ga, gb, gc) = jnp.split(proj, splits, axis=-1)

    def heads(t, nh):
        return t.reshape(b, s, nh, -1)

    o_a = dilated_attention(apply_rope(heads(qa, A_HEADS), cos, sin),
                            apply_rope(heads(ka, A_HEADS), cos, sin),
                            heads(va, A_HEADS)).reshape(b, s, A_W)
    o_b, _ = banded_attention(apply_rope(heads(qb, B_HEADS), cos, sin),
                              apply_rope(heads(kb, B_KV_HEADS), cos, sin),
                              heads(vb, B_KV_HEADS), B_WINDOW - 1, sink_b)
    o_b = o_b.reshape(b, s, B_QW)
    o_c = dsa_attention(apply_rope(heads(qc, C_HEADS), cos, sin),
                        apply_rope(heads(kc, 1), cos, sin)[:, :, 0],
                        vc,
                        apply_rope(heads(qi, IDX_HEADS), cos, sin),
                        apply_rope(heads(ki, 1), cos, sin)[:, :, 0],
                        wi).reshape(b, s, C_W)
    merged = (jax.nn.sigmoid(ga) * (o_a @ w_br_a)
              + jax.nn.sigmoid(gb) * (o_b @ w_br_b)
              + jax.nn.sigmoid(gc) * (o_c @ w_br_c))
    return merged @ w_out


def setup_inputs(seed: int = 0) -> dict:
    key = jax.random.key(seed)
    ks = jax.random.split(key, 16)
    f32 = jnp.float32

    def nrm(k, shape, scale):
        return jax.random.normal(k, shape, f32) * scale

    x = jax.random.normal(ks[0], (BATCH, SEQ, D_MODEL), f32)
    offs = jax.random.randint(ks[1], (BATCH, 1), 0, 4096, dtype=jnp.int32)
    positions = jnp.arange(SEQ, dtype=jnp.int32)[None, :] + offs
    return {
        "x": x,
        "positions": positions,
        "w_in": nrm(ks[2], (DEPTH, D_MODEL, D_IN), D_MODEL ** -0.5),
        "sink_b": nrm(ks[3], (DEPTH, B_HEADS), 0.5),
        "w_br_a": nrm(ks[4], (DEPTH, A_W, D_MODEL), A_W ** -0.5),
        "w_br_b": nrm(ks[5], (DEPTH, B_QW, D_MODEL), B_QW ** -0.5),
        "w_br_c": nrm(ks[6], (DEPTH, C_W, D_MODEL), C_W ** -0.5),
        "w_out": nrm(ks[7], (DEPTH, D_MODEL, D_MODEL), D_MODEL ** -0.5 * DN_BETA),
        "ffn1_in": nrm(ks[8], (DEPTH, D_MODEL, 2 * D_FF), D_MODEL ** -0.5),
        "ffn1_out": nrm(ks[9], (DEPTH, D_FF, D_MODEL), D_FF ** -0.5 * DN_BETA),
        "ffn2_in": nrm(ks[10], (DEPTH, D_MODEL, 2 * D_FF), D_MODEL ** -0.5),
        "ffn2_out": nrm(ks[11], (DEPTH, D_FF, D_MODEL), D_FF ** -0.5 * DN_BETA),
        "ln_g": 1.0 + nrm(ks[12], (DEPTH, 3, D_MODEL), 0.02),
        "ln_b": nrm(ks[13], (DEPTH, 3, D_MODEL), 0.02),
    }


def reference(x, positions, w_in, sink_b, w_br_a, w_br_b, w_br_c, w_out,
              ffn1_in, ffn1_out, ffn2_in, ffn2_out, ln_g, ln_b):
    cos, sin = rope_tables(positions, HEAD_DIM)
    for l in range(DEPTH):
        x = layer_norm(DN_ALPHA * x + 0.5 * swiglu(x, ffn1_in[l], ffn1_out[l]), ln_g[l, 0], ln_b[l, 0])
        x = layer_norm(DN_ALPHA * x + hybrid_mixer(x, cos, sin, w_in[l], sink_b[l], w_br_a[l],
                                                   w_br_b[l], w_br_c[l], w_out[l]),
                       ln_g[l, 1], ln_b[l, 1])
        x = layer_norm(DN_ALPHA * x + 0.5 * swiglu(x, ffn2_in[l], ffn2_out[l]), ln_g[l, 2], ln_b[l, 2])
    return x
```

```python
import contextlib
import numpy as np
import concourse.bass as bass
import concourse.mybir as mybir
from concourse.bass_utils import run_bass_kernel_spmd

F32 = mybir.dt.float32
BF16 = mybir.dt.bfloat16
I32 = mybir.dt.int32
AF = mybir.ActivationFunctionType
ALU = mybir.AluOpType
AX = mybir.AxisListType

D = 1024
S = 2048
NB = 16
L = 4
DFF = 2816
NF = 22
DIN = 6600
EPS = 1e-5
ALPHA = float((2 * L) ** 0.25)
NCORES = 8
SEQ_PER_CORE = 2

ENGS = ["pe", "act", "dve", "pool", "sp"]
NDMASEM = 8


class Op:
    __slots__ = ("eng", "emit", "deps", "idx", "needed", "count", "is_dma", "dsem", "dval", "slot_prev")

    def __init__(self, eng, emit, is_dma):
        self.eng = eng
        self.emit = emit
        self.deps = []
        self.idx = 0
        self.needed = False
        self.count = 0
        self.is_dma = is_dma
        self.dsem = None
        self.dval = 0
        self.slot_prev = None


def _region(ap):
    name = ap.tensor.name
    apl = ap.ap
    off = ap.offset
    if "dram" in str(ap.space).lower() or "hbm" in str(ap.space).lower():
        ext = 1
        for st, cnt in apl:
            ext += (cnt - 1) * abs(st)
        return (name, 0, 1, off, off + ext)
    row = apl[0][0]
    if row == 0:
        row = 1 << 40
    p_lo = off // row
    f_lo = off % row
    ext = 1
    for st, cnt in apl[1:]:
        ext += (cnt - 1) * abs(st)
    return (name, p_lo, p_lo + apl[0][1], f_lo, f_lo + ext)


def _overlap(a, b):
    return a[1] < b[2] and b[1] < a[2] and a[3] < b[4] and b[3] < a[4]


def _contains(a, b):
    return a[1] <= b[1] and a[2] >= b[2] and a[3] <= b[3] and a[4] >= b[4]


class Prog:
    def __init__(self, nc):
        self.nc = nc
        self.ops = {e: [] for e in ENGS}
        self.wr = {}
        self.rd = {}
        self.readonly = set()
        self.ndma = {e: 0 for e in ENGS}
        self.dma_last = {}

    def mark_readonly(self, name):
        self.readonly.add(name)

    def op(self, eng, emit, reads=(), writes=(), is_dma=False):
        o = Op(eng, emit, is_dma)
        o.idx = len(self.ops[eng])
        raw, other = [], []
        rregs = [_region(ap) for ap in reads]
        wregs = [_region(ap) for ap in writes]
        for r in rregs:
            for (x, p) in self.wr.get(r[0], ()):
                if _overlap(x, r):
                    raw.append(p)
        for r in wregs:
            for (x, p) in self.wr.get(r[0], ()):
                if _overlap(x, r):
                    other.append(p)
            for (x, p) in self.rd.get(r[0], ()):
                if _overlap(x, r):
                    other.append(p)
        best = {}

        def add(d):
            if d.is_dma:
                best[("dma", id(d))] = d
            else:
                k = d.eng
                if k not in best or best[k].idx < d.idx:
                    best[k] = d

        for d in raw:
            if (not d.is_dma) and d.eng == eng:
                if is_dma:
                    add(d)
                elif eng != "pe" and o.idx - d.idx <= 2:
                    add(d)
                continue
            add(d)
        for d in other:
            if (not d.is_dma) and d.eng == eng and not is_dma:
                continue
            add(d)
        o.deps = list(best.values())
        self.ops[eng].append(o)
        if is_dma:
            slot = self.ndma[eng] % NDMASEM
            self.ndma[eng] += 1
            o.slot_prev = self.dma_last.get((eng, slot))
            self.dma_last[(eng, slot)] = o
            o.dsem = (eng, slot)
            o.dval = (o.slot_prev.dval if o.slot_prev else 0) + 16
        for r in wregs:
            lst = self.wr.setdefault(r[0], [])
            lst[:] = [(x, p) for (x, p) in lst if not _contains(r, x)]
            lst.append((r, o))
            rl = self.rd.get(r[0])
            if rl:
                rl[:] = [(x, p) for (x, p) in rl if not _contains(r, x)]
        for r in rregs:
            if r[0] in self.readonly:
                continue
            rl = self.rd.setdefault(r[0], [])
            if not is_dma:
                rl[:] = [(x, p) for (x, p) in rl
                         if not ((not p.is_dma) and p.eng == eng and _contains(r, x))]
            rl.append((r, o))
        return o

    def mm(self, out, lhsT, rhs, start=True, stop=True):
        return self.op("pe", lambda e: e.matmul(out, lhsT, rhs, start=start, stop=stop),
                       reads=[lhsT, rhs], writes=[out])

    def transpose(self, out, in_, ident):
        return self.op("pe", lambda e: e.transpose(out, in_, ident), reads=[in_, ident], writes=[out])

    def dma(self, eng, out, in_, **kw):
        return self.op(eng, lambda e: e.dma_start(out=out, in_=in_, **kw), reads=[in_], writes=[out],
                       is_dma=True)

    def act(self, out, in_, func, bias=None, scale=None, accum_out=None, extra_reads=()):
        kw = {}
        rd = [in_] + list(extra_reads)
        if bias is not None:
            kw["bias"] = bias
            if not isinstance(bias, (int, float)):
                rd.append(bias)
        if scale is not None:
            kw["scale"] = scale
            if not isinstance(scale, (int, float)):
                rd.append(scale)
        wr = [out]
        if accum_out is not None:
            kw["accum_out"] = accum_out
            wr.append(accum_out)
        return self.op("act", lambda e: e.activation(out=out, in_=in_, func=func, **kw), reads=rd, writes=wr)

    def tt(self, eng, out, in0, in1, op):
        return self.op(eng, lambda e: e.tensor_tensor(out=out, in0=in0, in1=in1, op=op),
                       reads=[in0, in1], writes=[out])

    def ts(self, eng, out, in0, s1, s2, op0, op1=None, accum_out=None):
        rd = [in0]
        for s in (s1, s2):
            if s is not None and not isinstance(s, (int, float)):
                rd.append(s)
        wr = [out]
        kw = {}
        if op1 is not None:
            kw["op1"] = op1
        if accum_out is not None:
            kw["accum_out"] = accum_out
            wr.append(accum_out)
        return self.op(eng, lambda e: e.tensor_scalar(out=out, in0=in0, scalar1=s1, scalar2=s2, op0=op0, **kw),
                       reads=rd, writes=wr)

    def stt(self, out, in0, scalar, in1, op0, op1):
        rd = [in0, in1]
        if not isinstance(scalar, (int, float)):
            rd.append(scalar)
        return self.op("dve", lambda e: e.scalar_tensor_tensor(out=out, in0=in0, scalar=scalar, in1=in1,
                                                               op0=op0, op1=op1), reads=rd, writes=[out])

    def copy(self, eng, out, in_):
        if eng == "act":
            return self.op("act", lambda e: e.copy(out=out, in_=in_), reads=[in_], writes=[out])
        return self.op(eng, lambda e: e.tensor_copy(out=out, in_=in_), reads=[in_], writes=[out])

    def memset(self, eng, ap, val):
        return self.op(eng, lambda e: e.memset(ap, val), reads=[], writes=[ap])

    def barrier(self):
        lasts = []
        for e in ENGS:
            for o in reversed(self.ops[e]):
                if not o.is_dma and o.emit is not None:
                    lasts.append(o)
                    break
        dl = list(self.dma_last.values())
        for e in ENGS:
            o = Op(e, None, False)
            o.deps = list(lasts) + dl
            o.idx = len(self.ops[e])
            self.ops[e].append(o)
        self.wr.clear()
        self.rd.clear()

    def finalize(self, final_dmas):
        nc = self.nc
        for e in ENGS:
            for o in self.ops[e]:
                for d in o.deps:
                    d.needed = True
        cnt = {e: 0 for e in ENGS}
        for e in ENGS:
            for o in self.ops[e]:
                if o.needed and not o.is_dma and o.emit is not None:
                    cnt[e] += 1
                o.count = cnt[e]
        self.maxcount = dict(cnt)
        with contextlib.ExitStack() as st:
            csem = {e: st.enter_context(nc.semaphore("s_" + e)) for e in ENGS}
            dsem = {}
            for e in ENGS:
                for s in range(min(NDMASEM, self.ndma[e])):
                    dsem[(e, s)] = st.enter_context(nc.semaphore("d_%s_%d" % (e, s)))
            block = st.enter_context(nc.Block())
            handles = {"pe": block.tensor, "act": block.scalar, "dve": block.vector,
                       "pool": block.gpsimd, "sp": block.sync}
            prog = self

            def make(e):
                def body(eng):
                    waited = {}

                    def wait(key, sem, val):
                        if waited.get(key, 0) >= val:
                            return
                        waited[key] = val
                        eng.wait_ge(sem, val)

                    for o in prog.ops[e]:
                        for d in o.deps:
                            if d.is_dma:
                                wait(("d",) + d.dsem, dsem[d.dsem], d.dval)
                            elif d.count > 0:
                                wait(("c", d.eng), csem[d.eng], d.count)
                        if o.is_dma and o.slot_prev is not None:
                            wait(("d",) + o.dsem, dsem[o.dsem], o.slot_prev.dval)
                        if o.emit is None:
                            continue
                        ins = o.emit(eng)
                        if o.is_dma:
                            ins.then_inc(dsem[o.dsem], 16)
                        elif o.needed:
                            ins.then_inc(csem[e], 1)
                    if e == "sp":
                        for o in final_dmas:
                            wait(("d",) + o.dsem, dsem[o.dsem], o.dval)
                return body

            for e in ENGS:
                handles[e](make(e))


class Ctx:
    pass


NFM = 48
NTM = 768
NITER = 16
KSEL = 256
PI = float(np.pi)


def ln_block(P, C, l, i, t0, nt):
    tok = slice(t0, t0 + nt)
    for c in range(8):
        P.copy("act", C.zb[:, c, :nt], C.xT32[:, c, tok])
        P.act(C.zq[:, c, :nt], C.xT32[:, c, tok], AF.Square)
    for c in range(8):
        P.mm(C.bank[4][:, :nt], C.ones_b[:, :], C.zb[:, c, :nt], start=(c == 0), stop=(c == 7))
    for c in range(8):
        P.mm(C.bank[5][:, :nt], C.ones_b[:, :], C.zq[:, c, :nt], start=(c == 0), stop=(c == 7))
    mean, msq, rstd = C.st_a[:, :nt], C.st_b[:, :nt], C.st_c[:, :nt]
    P.act(mean, C.bank[4][:, :nt], AF.Copy, scale=1.0 / D)
    P.tt("dve", msq, mean, mean, ALU.mult)
    P.stt(rstd, C.bank[5][:, :nt], 1.0 / D, msq, ALU.mult, ALU.subtract)
    P.act(rstd, rstd, AF.Sqrt, bias=C.cst[:, 4:5])
    P.op("dve", lambda e: e.reciprocal(out=rstd, in_=rstd), reads=[rstd], writes=[rstd])
    P.tt("dve", msq, mean, rstd, ALU.mult)
    for c in range(8):
        x = C.xT32[:, c, tok]
        P.tt("dve", x, x, rstd, ALU.mult)
        P.tt("dve", x, x, msq, ALU.subtract)
        g = C.lng[:, l, i, c:c + 1]
        b = C.lnb[:, l, i, c:c + 1]
        P.ts("dve", x, x, g, b, ALU.mult, ALU.add)
        P.copy("act", C.xTb[:, c, tok], x)


def ffn_block(P, C, l, k):
    TC = 1024
    gT = C.ar1[:, :].rearrange("p (f t) -> p f t", f=NF)
    for tcx in range(S // TC):
        t0 = tcx * TC
        for f in range(NF):
            wb = C.win[C.win_i % 2]
            C.win_i += 1
            P.dma("pool", wb[:, :, :], C.d_ffn_in[l, k, f])
            for h in range(TC // 512):
                tok = slice(t0 + h * 512, t0 + (h + 1) * 512)
                loc = slice(h * 512, (h + 1) * 512)
                pg = C.bank[C.psg_i % 2]
                pu = C.bank[2 + C.psg_i % 2]
                C.psg_i += 1
                for c in range(8):
                    P.mm(pg[:, :], wb[:, c, 0:128], C.xTb[:, c, tok], start=(c == 0), stop=(c == 7))
                for c in range(8):
                    P.mm(pu[:, :], wb[:, c, 128:256], C.xTb[:, c, tok], start=(c == 0), stop=(c == 7))
                sg = C.sg[C.sg_i % 2]
                C.sg_i += 1
                P.act(sg[:, :], pg[:, :], AF.Silu)
                P.tt("dve", gT[:, f, loc], sg[:, :], pu[:, :], ALU.mult)
        for dc in range(8):
            wo = C.wout[C.wout_i % 2]
            C.wout_i += 1
            P.dma("pool", wo, C.d_ffn_out[l, k, dc])
            for h in range(TC // 512):
                tok = slice(t0 + h * 512, t0 + (h + 1) * 512)
                loc = slice(h * 512, (h + 1) * 512)
                py = C.bank[4 + C.psy_i % 2]
                C.psy_i += 1
                for f in range(NF):
                    P.mm(py[:, :], wo[:, f, :], gT[:, f, loc], start=(f == 0), stop=(f == NF - 1))
                xa = C.sg[C.sg_i % 2]
                C.sg_i += 1
                P.act(xa[:, :], C.xT32[:, dc, tok], AF.Copy, scale=ALPHA)
                P.stt(C.xT32[:, dc, tok], py[:, :], 0.5, xa[:, :], ALU.mult, ALU.add)
        for h in range(TC // 512):
            ln_block(P, C, l, 0 if k == 0 else 2, t0 + h * 512, 512)


def rope_tables(P, C, s):
    ang = C.f32s[:, 0:512]
    a2 = C.f32s[:, 512:1024]
    u = C.f32s[:, 1024:1536]
    r = C.f32s[:, 1536:2048]
    TWO_PI = 2 * PI

    def reduced_sin(dst, shift, post):
        P.ts("dve", a2, ang, shift, None, ALU.add)
        P.ts("dve", u, a2, 1.0 / TWO_PI, None, ALU.mult)
        P.copy("dve", C.posi[:, :], u)
        P.copy("dve", u, C.posi[:, :])
        P.stt(r, u, -TWO_PI, a2, ALU.mult, ALU.add)
        P.ts("dve", u, r, PI, -TWO_PI, ALU.is_gt, ALU.mult)
        P.tt("dve", r, r, u, ALU.add)
        P.ts("dve", u, r, -PI, TWO_PI, ALU.is_lt, ALU.mult)
        P.tt("dve", r, r, u, ALU.add)
        P.act(r, r, AF.Sin)
        P.ts("dve", dst, r, post, None, ALU.mult)

    for q in range(4):
        sl = slice(q * 512, (q + 1) * 512)
        P.dma("sp", C.posi[:, :], C.d_pos[s, :, sl])
        P.copy("dve", ang, C.posi[:, :])
        P.ts("dve", ang, ang, C.cst[:, 0:1], None, ALU.mult)
        reduced_sin(C.sinS[:, sl], 0.0, C.cst[:, 1:2])
        reduced_sin(C.cosT[:, sl], PI / 2, 1.0)


def fm_proj(P, C, l, n, dest, rope=True):
    wb = C.win[C.win_i % 2]
    C.win_i += 1
    P.dma("pool", wb[:, :, 0:128], C.d_wfm[l, n])
    for tg in range(4):
        tok = slice(tg * 512, (tg + 1) * 512)
        pq = C.bank[C.psg_i % 2]
        pr = C.bank[2 + C.psg_i % 2]
        C.psg_i += 1
        for c in range(8):
            P.mm(pq[:, :], wb[:, c, 0:128], C.xTb[:, c, tok], start=(c == 0), stop=(c == 7))
        if not rope:
            P.copy("act", dest[:, tok], pq[:, :])
            continue
        qb = C.qb16[C.sg_i % 2]
        P.copy("act", qb[:, :], pq[:, :])
        P.mm(pr[:, :], C.perm[:, :], qb[:, :])
        t1 = C.f32s[:, 2048:2560]
        t2 = C.sg[C.sg_i % 2]
        C.sg_i += 1
        P.op("dve", (lambda t2=t2, pq=pq, tok=tok: (lambda e: e.tensor_tensor(out=t2[:, :], in0=pq[:, :],
             in1=C.cosT[:, tok], op=ALU.mult)))(), reads=[pq[:, :], C.cosT[:, tok], qb[:, :]], writes=[t2[:, :]])
        P.tt("dve", t1, pr[:, :], C.sinS[:, tok], ALU.mult)
        P.tt("pool", dest[:, tok], t1, t2[:, :], ALU.add)


def tm_proj(P, C, l):
    wt = C.ar3[:, 0:8 * NTM].rearrange("p (c n) -> p c n", c=8)
    P.dma("pool", wt, C.d_wtm[l])
    P.memset("pool", C.VA[:, :, :, 64:65], 1.0)
    P.memset("pool", C.VB[:, :, :, 64:65], 1.0)
    P.memset("pool", C.VC[:, :, 64:65], 1.0)
    for tt in range(NB):
        tok = slice(tt * 128, (tt + 1) * 128)
        pa = C.bank[C.psg_i % 2]
        pb = C.bank[2 + C.psg_i % 2]
        C.psg_i += 1
        for c in range(8):
            P.mm(pa[:, :], C.xTb[:, c, tok], wt[:, c, 0:512], start=(c == 0), stop=(c == 7))
        for c in range(8):
            P.mm(pb[:, 0:200], C.xTb[:, c, tok], wt[:, c, 512:712], start=(c == 0), stop=(c == 7))
        P.copy("act", C.VA[:, tt, :, 0:64], pa[:, :].rearrange("p (h d) -> p h d", h=8))
        P.copy("dve", C.VB[:, tt, :, 0:64], pb[:, 0:128].rearrange("p (h d) -> p h d", h=2))
        P.copy("dve", C.VC[:, tt, 0:64], pb[:, 128:192])
        P.copy("dve", C.WI[:, tt, :], pb[:, 192:200])


def indexer_tile(P, C, i):
    Lk = (i + 1) * 128
    score = C.f32s[:, 0:2048]
    r = C.f32s[:, 2048:2560]
    for sb0 in range(0, Lk, 512):
        w = min(512, Lk - sb0)
        for h in range(8):
            b = (h % 2) * 64
            pi_ = C.bank[2 + C.psg_i % 2]
            C.psg_i += 1
            P.mm(pi_[:, 0:w], C.QI[b:b + 64, h // 2, i * 128:(i + 1) * 128], C.KI[b:b + 64, sb0:sb0 + w])
            wcol = C.WI[:, i, h:h + 1]
            if h == 0:
                P.ts("dve", score[:, sb0:sb0 + w], pi_[:, 0:w], 0.0, wcol, ALU.max, ALU.mult)
            else:
                P.act(r[:, 0:w], pi_[:, 0:w], AF.Relu)
                P.stt(score[:, sb0:sb0 + w], r[:, 0:w], wcol, score[:, sb0:sb0 + w], ALU.mult, ALU.add)
    dg = slice(i * 128, (i + 1) * 128)
    P.tt("dve", score[:, dg], score[:, dg], C.negm[:, :], ALU.add)
    lo, hi, mid, cnt, stp = (C.bis[:, k:k + 1] for k in range(5))
    W = C.bis[:, 8:8 + NITER]
    P.op("dve", lambda e: e.tensor_reduce(out=hi, in_=score[:, 0:Lk], axis=AX.X, op=ALU.max),
         reads=[score[:, 0:Lk]], writes=[hi])
    P.op("dve", lambda e: e.tensor_reduce(out=lo, in_=score[:, 0:i * 128], axis=AX.X, op=ALU.min),
         reads=[score[:, 0:i * 128]], writes=[lo])
    P.tt("dve", hi, hi, lo, ALU.subtract)
    P.ts("dve", W, C.cst[:, 8:8 + NITER], hi, None, ALU.mult)
    sel = C.sel[:, 0:Lk]
    for k in range(NITER):
        P.tt("dve", mid, lo, W[:, k:k + 1], ALU.add)
        P.ts("dve", sel, score[:, 0:Lk], mid, 0.0, ALU.is_ge, ALU.add, accum_out=cnt)
        P.ts("dve", stp, cnt, KSEL - 0.5, W[:, k:k + 1], ALU.is_ge, ALU.mult)
        P.tt("dve", lo, lo, stp, ALU.add)
    P.ts("dve", sel, score[:, 0:Lk], lo, None, ALU.is_ge)


def attention_c(P, C):
    selT = C.bank6b[:, 0:256].rearrange("p (b q) -> p b q", b=2)

    class St:
        k = 0
    for i in range(NB):
        if i >= 2:
            indexer_tile(P, C, i)

        def maskf(i_, j, i=i):
            if i < 2:
                return C.cmask[:, 5, :] if j == i else None
            buf = selT[:, St.k % 2, :]
            St.k += 1
            P.transpose(buf, C.sel[:, j * 128:(j + 1) * 128], C.ident[:, :])
            return buf
        attention_one(P, C, i, maskf)


def attention_one(P, C, i, maskf):
    attention_tiles(P, C, [i], lambda h: C.Q[(h % 2) * 64:(h % 2) * 64 + 64, h // 2, :],
                    lambda h: C.KC[(h % 2) * 64:(h % 2) * 64 + 64, :],
                    lambda h, j: C.VC[:, j, 0:65], lambda i_: list(range(i_ + 1)), maskf, 2)


def attention_tiles(P, C, tiles, qview, kview, vview, jlist, maskf, br, sinkexp=None):
    for i in tiles:
        qs = slice(i * 128, (i + 1) * 128)
        js = jlist(i)
        po = [C.bank[4], C.bank[5]]
        for jn, j in enumerate(js):
            ks = slice(j * 128, (j + 1) * 128)
            m = maskf(i, j)
            kk = C.pss_i % 2
            C.pss_i += 1
            for par in range(2):
                pss = C.bank[2 * kk + par]
                pv = pss[:, :].rearrange("p (h q) -> p h q", h=4)
                for hh in range(4):
                    h = 2 * hh + par
                    P.mm(pv[:, hh, :], kview(h)[:, ks], qview(h)[:, qs])
                e = C.ebuf[par]
                P.act(e[:, :], pss[:, :], AF.Exp, scale=0.125)
                ev = e[:, :].rearrange("p (h q) -> p h q", h=4)
                if m is not None:
                    mb = m.unsqueeze(1).broadcast_to([128, 4, 128])
                    P.tt("dve", ev, ev, mb, ALU.mult)
                for hh in range(4):
                    h = 2 * hh + par
                    P.mm(po[par][:, hh * 65:(hh + 1) * 65], ev[:, hh, :], vview(h, j),
                         start=(jn == 0 and hh == 0), stop=(jn == len(js) - 1 and hh == 3))
        den = C.den[:, :]
        for par in range(2):
            dv = po[par][:, 0:260].rearrange("p (h d) -> p h d", h=4)[:, :, 64]
            if sinkexp is not None:
                sk = sinkexp[:, :].rearrange("p (hh g) -> p g hh", g=2)[:, par, :]
                P.tt("dve", den[:, par * 4:(par + 1) * 4], dv, sk, ALU.add)
            else:
                P.copy("dve", den[:, par * 4:(par + 1) * 4], dv)
        P.op("dve", lambda e_: e_.reciprocal(out=den, in_=den), reads=[den], writes=[den])
        on = C.on[:, :, :]
        for h in range(8):
            par, hh = h % 2, h // 2
            P.act(on[:, h, :], po[par][:, hh * 65:hh * 65 + 64], AF.Copy,
                  scale=den[:, par * 4 + hh:par * 4 + hh + 1])
        ptr = C.bank7[:, 0:512].rearrange("p (c q) -> p c q", c=4)
        for c in range(4):
            P.transpose(ptr[:, c, :], on[:, 2 * c:2 * c + 2, :].rearrange("p h d -> p (h d)"), C.ident[:, :])
        ot = C.otst[0]
        P.copy("dve", ot[:, :, :], ptr)
        P.dma("sp", C.d_oT[br, :, :, qs], ot[:, :, :])


def merge_block(P, C, l):
    oTc = C.ar1[:, 0:6144].rearrange("p (x c t) -> p x c t", x=3, c=4)
    mT = C.ar1[:, 6144:10240].rearrange("p (m t) -> p m t", m=8)
    wo = C.ar1[:, 10240:12288].rearrange("p (i m j) -> p i m j", i=2, m=8)
    mg = C.f32s[:, 1024:1536]
    tmp = C.f32s[:, 1536:2048]
    for tg in range(4):
        tok = slice(tg * 512, (tg + 1) * 512)
        for x in range(3):
            P.dma("sp", oTc[:, x, :, :], C.d_oT[x, :, :, tok])
        for mc in range(8):
            for x in range(3):
                wb = C.win[C.win_i % 2]
                C.win_i += 1
                P.dma("pool", wb[:, :, 0:128], C.d_wfm[l, 24 + x * 8 + mc])
                P.dma("pool", wb[:, 0:4, 128:256], C.d_wbr[l, x, mc])
                pg = C.bank[C.psg_i % 2]
                pb = C.bank[2 + C.psg_i % 2]
                C.psg_i += 1
                for c in range(8):
                    P.mm(pg[:, :], wb[:, c, 0:128], C.xTb[:, c, tok], start=(c == 0), stop=(c == 7))
                for cc in range(4):
                    P.mm(pb[:, :], wb[:, cc, 128:256], oTc[:, x, cc, :], start=(cc == 0), stop=(cc == 3))
                sg = C.sg[C.sg_i % 2]
                C.sg_i += 1
                P.act(sg[:, :], pg[:, :], AF.Sigmoid)
                if x == 0:
                    P.tt("dve", mg, sg[:, :], pb[:, :], ALU.mult)
                else:
                    P.tt("dve", tmp, sg[:, :], pb[:, :], ALU.mult)
                    P.tt("pool", mg, mg, tmp, ALU.add)
            P.copy("act", mT[:, mc, :], mg)
        for dc in range(8):
            w = wo[:, C.wo_i % 2, :, :]
            C.wo_i += 1
            P.dma("pool", w, C.d_wo[l, dc])
            py = C.bank[4 + C.psy_i % 2]
            C.psy_i += 1
            for mc in range(8):
                P.mm(py[:, :], w[:, mc, :], mT[:, mc, :], start=(mc == 0), stop=(mc == 7))
            xa = C.sg[C.sg_i % 2]
            C.sg_i += 1
            P.act(xa[:, :], C.xT32[:, dc, tok], AF.Copy, scale=ALPHA)
            P.stt(C.xT32[:, dc, tok], py[:, :], 1.0, xa[:, :], ALU.mult, ALU.add)
        ln_block(P, C, l, 1, tg * 512, 512)


def mixer_block(P, C, l, s, dbg=None):
    import os
    stop = os.environ.get("MIXSTOP", "")
    tm_proj(P, C, l)
    rope_tables(P, C, s)
    if stop == "tm":
        return
    hb = lambda h: slice((h % 2) * 64, (h % 2) * 64 + 64)
    for n in range(4):
        fm_proj(P, C, l, n, C.Q[:, n, :])
        fm_proj(P, C, l, 4 + n, C.K[:, n, :])

    def maskA(i, j):
        o = i - j
        return C.cmask[:, {0: 0, 1: 1, 2: 2, 3: 2, 4: 3}.get(o, 4), :]
    attention_tiles(P, C, range(NB), lambda h: C.Q[hb(h), h // 2, :], lambda h: C.K[hb(h), h // 2, :],
                    lambda h, j: C.VA[:, j, h, 0:65], lambda i: list(range(i + 1)), maskA, 0)
    if stop == "A":
        return
    for n in range(4):
        fm_proj(P, C, l, 8 + n, C.Q[:, n, :])
    fm_proj(P, C, l, 12, C.K[:, 0, :])
    fm_proj(P, C, l, 13, C.K[:, 1, :])
    P.dma("sp", C.sinkexp[:, :], C.d_sink[l])
    P.act(C.sinkexp[:, :], C.sinkexp[:, :], AF.Exp)

    def kB(h):
        gk = h // 4
        return C.K[hb(h), 0 if gk == (h % 2) else 1, :]
    attention_tiles(P, C, range(NB), lambda h: C.Q[hb(h), h // 2, :], kB,
                    lambda h, j: C.VB[:, j, h // 4, 0:65], lambda i: ([i - 1, i] if i > 0 else [0]),
                    lambda i, j: C.cmask[:, 5 if j == i else 6, :], 1, sinkexp=C.sinkexp)
    if stop == "B":
        return
    for n in range(4):
        fm_proj(P, C, l, 14 + n, C.Q[:, n, :])
    fm_proj(P, C, l, 18, C.K[:, 0, :])
    for n in range(4):
        fm_proj(P, C, l, 19 + n, C.QI[:, n, :])
    fm_proj(P, C, l, 23, C.K[:, 1, :])
    C.KC = C.K[:, 0, :]
    C.KI = C.K[:, 1, :]
    attention_c(P, C)
    if stop == "C":
        return
    merge_block(P, C, l)


def build_program(n_layers=L, stages=("ffn1", "mixer", "ffn2"), n_seq=SEQ_PER_CORE, dbg=False):
    nc = bass.Bass("TRN2", target_bir_lowering=False)
    C = Ctx()

    def din(name, shape, dt=F32):
        return nc.dram_tensor(name, shape, dt, kind="ExternalInput").ap()
    C.d_xT = din("xT", [SEQ_PER_CORE, 128, 8, S])
    C.d_pos = din("posb", [SEQ_PER_CORE, 128, S], I32)
    C.d_ffn_in = din("ffn_in", [L, 2, NF, 128, 8, 256])
    C.d_ffn_out = din("ffn_out", [L, 2, 8, 128, NF, 128])
    C.d_lng = din("lng", [128, L, 3, 8])
    C.d_lnb = din("lnb", [128, L, 3, 8])
    C.d_wfm = din("wfm", [L, NFM, 128, 8, 128])
    C.d_wtm = din("wtm", [L, 128, 8, NTM])
    C.d_wbr = din("wbr", [L, 3, 8, 128, 4, 128])
    C.d_wo = din("wo", [L, 8, 128, 8, 128])
    C.d_sink = din("sink", [L, 128, 8])
    C.d_cst = din("cst", [128, 32])
    C.d_cmask = din("cmask", [128, 7, 128])
    C.d_negm = din("negm", [128, 128])
    C.d_pi = din("permident", [128, 2, 128])
    C.d_out = nc.dram_tensor("outT", [SEQ_PER_CORE, 128, 8, S], F32, kind="ExternalOutput").ap()
    C.d_oT = nc.dram_tensor("oT_scr", [3, 128, 4, S], BF16, kind=("ExternalOutput" if dbg else "Internal")).ap()
    P = Prog(nc)
    for n in ("xT", "posb", "ffn_in", "ffn_out", "lng", "lnb", "wfm", "wtm", "wbr", "wo", "sink", "cst",
              "cmask", "negm", "permident"):
        P.mark_readonly(n)
    with contextlib.ExitStack() as st:
        def sb(name, shape, dt):
            return st.enter_context(nc.sbuf_tensor(name, shape, dt))

        C.xT32 = sb("xT32", [128, 8, S], F32)
        C.xTb = sb("xTb", [128, 8, S], BF16)
        C.ar1 = sb("ar1", [128, NF * 1024], BF16)
        C.ar2 = sb("ar2", [128, 8448], BF16)
        C.ar3 = sb("ar3", [128, 6144], BF16)
        C.f32s = sb("f32s", [128, 2560], F32)
        C.win = [sb("win0", [128, 8, 256], BF16)] * 2
        C.lng = sb("lng_s", [128, L, 3, 8], F32)
        C.lnb = sb("lnb_s", [128, L, 3, 8], F32)
        C.ones_b = sb("ones_b", [128, 128], BF16)
        C.cst = sb("cst_s", [128, 32], F32)
        C.cmask = sb("cmask_s", [128, 7, 128], BF16)
        C.negm = sb("negm_s", [128, 128], F32)
        C.pi = sb("pi_s", [128, 2, 128], BF16)
        C.qb16 = [sb("qb16_0", [128, 512], BF16)] * 2
        C.ebuf = [sb("ebuf%d" % i, [128, 512], BF16) for i in range(2)]
        C.on = sb("on", [128, 8, 64], BF16)
        C.otst = [sb("otst0", [128, 4, 128], BF16)]
        C.WI = sb("WI", [128, NB, 8], F32)
        C.den = sb("den", [128, 8], F32)
        C.sinkexp = sb("sinkexp", [128, 8], F32)
        C.bis = sb("bis", [128, 32], F32)
        C.bank = [st.enter_context(nc.psum_tensor("bank%d" % i, [128, 512], F32)) for i in range(6)]
        C.bank6b = st.enter_context(nc.psum_tensor("bank6b", [128, 1024], BF16))
        C.bank7 = st.enter_context(nc.psum_tensor("bank7", [128, 1024], BF16))
        C.cosT = C.ar3[:, 0:2048]
        C.sinS = C.ar3[:, 2048:4096]
        C.posi = C.f32s[:, 2048:2560].bitcast(I32)
        C.perm = C.pi[:, 0, :]
        C.ident = C.pi[:, 1, :]
        C.sg = [C.f32s[:, 0:512], C.f32s[:, 512:1024]]
        C.st_a = C.f32s[:, 1024:1536]
        C.st_b = C.f32s[:, 1536:2048]
        C.st_c = C.f32s[:, 2048:2560]
        C.zb = C.ar2[:, 0:4096].rearrange("p (c t) -> p c t", c=8)
        C.zq = C.ar2[:, 4096:8192].rearrange("p (c t) -> p c t", c=8)
        C.wout = [C.ar3[:, i * 2816:(i + 1) * 2816].rearrange("p (f j) -> p f j", f=NF) for i in range(2)]
        C.Q = C.ar1[:, 0:8192].rearrange("p (c t) -> p c t", c=4)
        C.K = C.ar1[:, 8192:16384].rearrange("p (c t) -> p c t", c=4)
        C.VB = C.ar1[:, 16384:18496].rearrange("p (t h d) -> p t h d", t=NB, h=2)
        C.VC = C.ar1[:, 18496:19552].rearrange("p (t d) -> p t d", t=NB)
        C.sel = C.ar1[:, 19552:21600]
        C.VA = C.ar2[:, 0:8448].rearrange("p (t h d) -> p t h d", t=NB, h=8)
        C.QI = C.ar2[:, 0:8192].rearrange("p (c t) -> p c t", c=4)
        C.win_i = C.wout_i = C.psg_i = C.psy_i = C.sg_i = C.pss_i = C.e_i = C.ot_i = C.wo_i = 0

        P.dma("sp", C.lng[:, :, :, :], C.d_lng)
        P.dma("sp", C.lnb[:, :, :, :], C.d_lnb)
        P.dma("sp", C.cst[:, :], C.d_cst)
        P.dma("sp", C.negm[:, :], C.d_negm)
        P.dma("pool", C.cmask[:, :, :], C.d_cmask)
        P.dma("pool", C.pi[:, :, :], C.d_pi)
        P.memset("dve", C.ones_b[:, :], 1.0)
        finals = []
        for s in range(n_seq):
            for c in range(8):
                P.dma("sp", C.xT32[:, c, :], C.d_xT[s, :, c, :])
            for c in range(8):
                P.copy("act", C.xTb[:, c, :], C.xT32[:, c, :])
            for l in range(n_layers):
                if "ffn1" in stages:
                    ffn_block(P, C, l, 0)
                if "mixer" in stages:
                    mixer_block(P, C, l, s)
                if "ffn2" in stages:
                    ffn_block(P, C, l, 1)
            for c in range(8):
                finals.append(P.dma("sp", C.d_out[s, :, c, :], C.xT32[:, c, :]))
        P.finalize(finals)
    C.P = P
    return nc, C


_OFF = {}
_o = 0
for _n, _w in (("qa", 512), ("ka", 512), ("va", 512), ("qb", 512), ("kb", 128), ("vb", 128), ("qc", 512),
               ("kc", 64), ("vc", 64), ("qi", 512), ("ki", 64), ("wi", 8), ("ga", 1024), ("gb", 1024),
               ("gc", 1024)):
    _OFF[_n] = _o
    _o += _w


def _fm_cols():
    cols = []
    r = lambda a, n: list(range(a, a + n))
    for n in range(4):
        cols.append(r(_OFF["qa"] + 128 * n, 128))
    for n in range(4):
        cols.append(r(_OFF["ka"] + 128 * n, 128))
    for n in range(4):
        cols.append(r(_OFF["qb"] + 128 * n, 128))
    cols.append(r(_OFF["kb"], 128))
    cols.append(r(_OFF["kb"] + 64, 64) + r(_OFF["kb"], 64))
    for n in range(4):
        cols.append(r(_OFF["qc"] + 128 * n, 128))
    cols.append(r(_OFF["kc"], 64) * 2)
    for n in range(4):
        cols.append(r(_OFF["qi"] + 128 * n, 128))
    cols.append(r(_OFF["ki"], 64) * 2)
    for g in ("ga", "gb", "gc"):
        for n in range(8):
            cols.append(r(_OFF[g] + 128 * n, 128))
    return np.array(cols)


def host_consts():
    f32 = np.float32
    p = np.arange(128)
    cst = np.zeros((128, 32), f32)
    cst[:, 0] = (10000.0 ** (-(2.0 * (p % 32)) / 64.0)).astype(f32)
    cst[:, 1] = np.where((p % 64) < 32, -1.0, 1.0)
    cst[:, 2] = -np.pi
    cst[:, 4] = EPS
    cst[:, 8:8 + NITER] = (0.5 ** (np.arange(NITER) + 1))[None, :]
    k = p[:, None]
    q = p[None, :]

    def mult(delta):
        m = ((delta >= 0) & (delta <= 128)).astype(f32)
        m += ((delta >= 0) & (delta <= 512) & (delta % 4 == 0))
        m += ((delta >= 0) & (delta <= 2048) & (delta % 16 == 0))
        return m
    cm = np.zeros((128, 7, 128), f32)
    for idx, o in enumerate((0, 1, 2, 4, 5)):
        cm[:, idx, :] = mult(128 * o + q - k)
    cm[:, 5, :] = (k <= q)
    cm[:, 6, :] = (k > q)
    negm = np.where(p[None, :] <= p[:, None], 0.0, -1e30).astype(f32)
    perm = np.zeros((128, 128), f32)
    partner = np.where((p % 64) < 32, p + 32, p - 32)
    perm[partner, p] = 1.0
    pi = np.stack([perm, np.eye(128, dtype=f32)], 1)
    return {"cst": cst, "cmask": cm, "negm": negm, "permident": np.ascontiguousarray(pi)}


def host_layout(inputs):
    f32 = np.float32
    x = np.asarray(inputs["x"], f32)
    B = x.shape[0]
    xT = np.ascontiguousarray(x.reshape(B, S, 8, 128).transpose(0, 3, 2, 1))
    pos = np.asarray(inputs["positions"]).astype(np.int32)
    posb = np.ascontiguousarray(np.broadcast_to(pos[:, None, :], (B, 128, S)))
    shared = host_consts()
    fin = np.stack([np.asarray(inputs["ffn1_in"], f32), np.asarray(inputs["ffn2_in"], f32)], 1)
    g = fin[..., :DFF].reshape(L, 2, 8, 128, NF, 128)
    u = fin[..., DFF:].reshape(L, 2, 8, 128, NF, 128)
    gu = np.concatenate([g, u], axis=-1)
    shared["ffn_in"] = np.ascontiguousarray(gu.transpose(0, 1, 4, 3, 2, 5))
    fo = np.stack([np.asarray(inputs["ffn1_out"], f32), np.asarray(inputs["ffn2_out"], f32)], 1)
    fo = fo.reshape(L, 2, NF, 128, 8, 128)
    shared["ffn_out"] = np.ascontiguousarray(fo.transpose(0, 1, 4, 3, 2, 5))
    shared["lng"] = np.ascontiguousarray(np.asarray(inputs["ln_g"], f32).reshape(L, 3, 8, 128).transpose(3, 0, 1, 2))
    shared["lnb"] = np.ascontiguousarray(np.asarray(inputs["ln_b"], f32).reshape(L, 3, 8, 128).transpose(3, 0, 1, 2))
    w_in = np.asarray(inputs["w_in"], f32)
    fm = w_in[:, :, _fm_cols()]
    fm = fm.reshape(L, 8, 128, NFM, 128)
    shared["wfm"] = np.ascontiguousarray(fm.transpose(0, 3, 2, 1, 4))
    tmc = np.concatenate([np.arange(_OFF["va"], _OFF["va"] + 512), np.arange(_OFF["vb"], _OFF["vb"] + 128),
                          np.arange(_OFF["vc"], _OFF["vc"] + 64), np.arange(_OFF["wi"], _OFF["wi"] + 8)])
    tm = np.zeros((L, D, NTM), f32)
    tm[:, :, :712] = w_in[:, :, tmc]
    tm = tm.reshape(L, 8, 128, NTM)
    shared["wtm"] = np.ascontiguousarray(tm.transpose(0, 2, 1, 3))
    wbr = np.stack([np.asarray(inputs[k], f32) for k in ("w_br_a", "w_br_b", "w_br_c")], 1)
    wbr = wbr.reshape(L, 3, 4, 128, 8, 128)
    shared["wbr"] = np.ascontiguousarray(wbr.transpose(0, 1, 4, 3, 2, 5))
    wo = np.asarray(inputs["w_out"], f32).reshape(L, 8, 128, 8, 128)
    shared["wo"] = np.ascontiguousarray(wo.transpose(0, 3, 2, 1, 4))
    sink = np.asarray(inputs["sink_b"], f32)
    shared["sink"] = np.ascontiguousarray(np.broadcast_to(sink[:, None, :], (L, 128, 8)))
    in_maps = []
    for c in range(NCORES):
        m = dict(shared)
        m["xT"] = np.ascontiguousarray(xT[c * SEQ_PER_CORE:(c + 1) * SEQ_PER_CORE])
        m["posb"] = np.ascontiguousarray(posb[c * SEQ_PER_CORE:(c + 1) * SEQ_PER_CORE])
        in_maps.append(m)
    return in_maps


def gather(results):
    outs = [r["outT"] for r in results]
    o = np.concatenate(outs, 0)
    B = o.shape[0]
    return np.ascontiguousarray(o.transpose(0, 3, 2, 1).reshape(B, S, D)).astype(np.float32)


def kernel(**inputs):
    in_maps = host_layout(inputs)
    nc, _ = build_program()
    res = run_bass_kernel_spmd(nc, in_maps, core_ids=list(range(NCORES)))
    return gather(res.results)
```

```python
import contextlib
import numpy as np
import concourse.bass as bass
import concourse.mybir as mybir
from concourse.bass_utils import run_bass_kernel_spmd

F32 = mybir.dt.float32
BF16 = mybir.dt.bfloat16
I32 = mybir.dt.int32
AF = mybir.ActivationFunctionType
ALU = mybir.AluOpType
AX = mybir.AxisListType

D = 1024
S = 2048
NB = 16
L = 4
DFF = 2816
NF = 22
DIN = 6600
EPS = 1e-5
ALPHA = float((2 * L) ** 0.25)
NCORES = 8
SEQ_PER_CORE = 2

ENGS = ["pe", "act", "dve", "pool", "sp"]
NDMASEM = 8


class Op:
    __slots__ = ("eng", "emit", "deps", "idx", "needed", "count", "is_dma", "dsem", "dval", "slot_prev")

    def __init__(self, eng, emit, is_dma):
        self.eng = eng
        self.emit = emit
        self.deps = []
        self.idx = 0
        self.needed = False
        self.count = 0
        self.is_dma = is_dma
        self.dsem = None
        self.dval = 0
        self.slot_prev = None


def _region(ap):
    name = ap.tensor.name
    apl = ap.ap
    off = ap.offset
    if "dram" in str(ap.space).lower() or "hbm" in str(ap.space).lower():
        ext = 1
        for st, cnt in apl:
            ext += (cnt - 1) * abs(st)
        return (name, 0, 1, off, off + ext)
    if name.startswith("bank"):
        return (name, 0, 128, 0, 1 << 40)
    row = apl[0][0]
    if row == 0:
        row = 1 << 40
    p_lo = off // row
    f_lo = off % row
    ext = 1
    for st, cnt in apl[1:]:
        ext += (cnt - 1) * abs(st)
    return (name, p_lo, p_lo + apl[0][1], f_lo, f_lo + ext)


def _overlap(a, b):
    return a[1] < b[2] and b[1] < a[2] and a[3] < b[4] and b[3] < a[4]


def _contains(a, b):
    return a[1] <= b[1] and a[2] >= b[2] and a[3] <= b[3] and a[4] >= b[4]


class Prog:
    def __init__(self, nc):
        self.nc = nc
        self.ops = {e: [] for e in ENGS}
        self.wr = {}
        self.rd = {}
        self.readonly = set()
        self.ndma = {e: 0 for e in ENGS}
        self.dma_last = {}

    def mark_readonly(self, name):
        self.readonly.add(name)

    def op(self, eng, emit, reads=(), writes=(), is_dma=False):
        o = Op(eng, emit, is_dma)
        o.idx = len(self.ops[eng])
        raw, other = [], []
        rregs = [_region(ap) for ap in reads]
        wregs = [_region(ap) for ap in writes]
        for r in rregs:
            for (x, p) in self.wr.get(r[0], ()):
                if _overlap(x, r):
                    raw.append(p)
        for r in wregs:
            for (x, p) in self.wr.get(r[0], ()):
                if _overlap(x, r):
                    other.append(p)
            for (x, p) in self.rd.get(r[0], ()):
                if _overlap(x, r):
                    other.append(p)
        best = {}

        def add(d):
            if d.is_dma:
                best[("dma", id(d))] = d
            else:
                k = d.eng
                if k not in best or best[k].idx < d.idx:
                    best[k] = d

        for d in raw:
            if (not d.is_dma) and d.eng == eng:
                if is_dma:
                    add(d)
                elif eng != "pe" and o.idx - d.idx <= 2:
                    add(d)
                continue
            add(d)
        for d in other:
            if (not d.is_dma) and d.eng == eng and not is_dma:
                continue
            add(d)
        o.deps = list(best.values())
        self.ops[eng].append(o)
        if is_dma:
            slot = self.ndma[eng] % NDMASEM
            self.ndma[eng] += 1
            o.slot_prev = self.dma_last.get((eng, slot))
            self.dma_last[(eng, slot)] = o
            o.dsem = (eng, slot)
            o.dval = (o.slot_prev.dval if o.slot_prev else 0) + 16
        for r in wregs:
            lst = self.wr.setdefault(r[0], [])
            lst[:] = [(x, p) for (x, p) in lst if not _contains(r, x)]
            lst.append((r, o))
            rl = self.rd.get(r[0])
            if rl:
                rl[:] = [(x, p) for (x, p) in rl if not _contains(r, x)]
        for r in rregs:
            if r[0] in self.readonly:
                continue
            rl = self.rd.setdefault(r[0], [])
            if not is_dma:
                rl[:] = [(x, p) for (x, p) in rl
                         if not ((not p.is_dma) and p.eng == eng and _contains(r, x))]
            rl.append((r, o))
        return o

    def mm(self, out, lhsT, rhs, start=True, stop=True):
        return self.op("pe", lambda e: e.matmul(out, lhsT, rhs, start=start, stop=stop),
                       reads=[lhsT, rhs], writes=[out])

    def transpose(self, out, in_, ident):
        return self.op("pe", lambda e: e.transpose(out, in_, ident), reads=[in_, ident], writes=[out])

    def dma(self, eng, out, in_, **kw):
        return self.op(eng, lambda e: e.dma_start(out=out, in_=in_, **kw), reads=[in_], writes=[out],
                       is_dma=True)

    def act(self, out, in_, func, bias=None, scale=None, accum_out=None, extra_reads=()):
        kw = {}
        rd = [in_] + list(extra_reads)
        if bias is not None:
            kw["bias"] = bias
            if not isinstance(bias, (int, float)):
                rd.append(bias)
        if scale is not None:
            kw["scale"] = scale
            if not isinstance(scale, (int, float)):
                rd.append(scale)
        wr = [out]
        if accum_out is not None:
            kw["accum_out"] = accum_out
            wr.append(accum_out)
        return self.op("act", lambda e: e.activation(out=out, in_=in_, func=func, **kw), reads=rd, writes=wr)

    def tt(self, eng, out, in0, in1, op):
        return self.op(eng, lambda e: e.tensor_tensor(out=out, in0=in0, in1=in1, op=op),
                       reads=[in0, in1], writes=[out])

    def ts(self, eng, out, in0, s1, s2, op0, op1=None, accum_out=None):
        rd = [in0]
        for s in (s1, s2):
            if s is not None and not isinstance(s, (int, float)):
                rd.append(s)
        wr = [out]
        kw = {}
        if op1 is not None:
            kw["op1"] = op1
        if accum_out is not None:
            kw["accum_out"] = accum_out
            wr.append(accum_out)
        return self.op(eng, lambda e: e.tensor_scalar(out=out, in0=in0, scalar1=s1, scalar2=s2, op0=op0, **kw),
                       reads=rd, writes=wr)

    def stt(self, out, in0, scalar, in1, op0, op1):
        rd = [in0, in1]
        if not isinstance(scalar, (int, float)):
            rd.append(scalar)
        return self.op("dve", lambda e: e.scalar_tensor_tensor(out=out, in0=in0, scalar=scalar, in1=in1,
                                                               op0=op0, op1=op1), reads=rd, writes=[out])

    def copy(self, eng, out, in_):
        if eng == "act":
            return self.op("act", lambda e: e.copy(out=out, in_=in_), reads=[in_], writes=[out])
        return self.op(eng, lambda e: e.tensor_copy(out=out, in_=in_), reads=[in_], writes=[out])

    def memset(self, eng, ap, val):
        return self.op(eng, lambda e: e.memset(ap, val), reads=[], writes=[ap])

    def barrier(self):
        lasts = []
        for e in ENGS:
            for o in reversed(self.ops[e]):
                if not o.is_dma and o.emit is not None:
                    lasts.append(o)
                    break
        dl = list(self.dma_last.values())
        for e in ENGS:
            o = Op(e, None, False)
            o.deps = list(lasts) + dl
            o.idx = len(self.ops[e])
            self.ops[e].append(o)
        self.wr.clear()
        self.rd.clear()

    def finalize(self, final_dmas):
        nc = self.nc
        for e in ENGS:
            for o in self.ops[e]:
                for d in o.deps:
                    d.needed = True
        cnt = {e: 0 for e in ENGS}
        for e in ENGS:
            for o in self.ops[e]:
                if o.needed and not o.is_dma and o.emit is not None:
                    cnt[e] += 1
                o.count = cnt[e]
        self.maxcount = dict(cnt)
        with contextlib.ExitStack() as st:
            csem = {e: st.enter_context(nc.semaphore("s_" + e)) for e in ENGS}
            dsem = {}
            for e in ENGS:
                for s in range(min(NDMASEM, self.ndma[e])):
                    dsem[(e, s)] = st.enter_context(nc.semaphore("d_%s_%d" % (e, s)))
            block = st.enter_context(nc.Block())
            handles = {"pe": block.tensor, "act": block.scalar, "dve": block.vector,
                       "pool": block.gpsimd, "sp": block.sync}
            prog = self

            def make(e):
                def body(eng):
                    waited = {}

                    def wait(key, sem, val):
                        if waited.get(key, 0) >= val:
                            return
                        waited[key] = val
                        eng.wait_ge(sem, val)

                    for o in prog.ops[e]:
                        for d in o.deps:
                            if d.is_dma:
                                wait(("d",) + d.dsem, dsem[d.dsem], d.dval)
                            elif d.count > 0:
                                wait(("c", d.eng), csem[d.eng], d.count)
                        if o.is_dma and o.slot_prev is not None:
                            wait(("d",) + o.dsem, dsem[o.dsem], o.slot_prev.dval)
                        if o.emit is None:
                            continue
                        ins = o.emit(eng)
                        if o.is_dma:
                            ins.then_inc(dsem[o.dsem], 16)
                        elif o.needed:
                            ins.then_inc(csem[e], 1)
                    if e == "sp":
                        for o in final_dmas:
                            wait(("d",) + o.dsem, dsem[o.dsem], o.dval)
                return body

            for e in ENGS:
                handles[e](make(e))


class Ctx:
    pass


NFM = 48
NTM = 768
NITER = 16
KSEL = 256
PI = float(np.pi)


def ln_block(P, C, l, i, t0, nt):
    tok = slice(t0, t0 + nt)
    for c in range(8):
        P.copy("act", C.zb[:, c, :nt], C.xT32[:, c, tok])
        P.act(C.zq[:, c, :nt], C.xT32[:, c, tok], AF.Square)
    for c in range(8):
        P.mm(C.bank[4][:, :nt], C.ones_b[:, :], C.zb[:, c, :nt], start=(c == 0), stop=(c == 7))
    for c in range(8):
        P.mm(C.bank[5][:, :nt], C.ones_b[:, :], C.zq[:, c, :nt], start=(c == 0), stop=(c == 7))
    mean, msq, rstd = C.st_a[:, :nt], C.st_b[:, :nt], C.st_c[:, :nt]
    P.act(mean, C.bank[4][:, :nt], AF.Copy, scale=1.0 / D)
    P.tt("dve", msq, mean, mean, ALU.mult)
    P.stt(rstd, C.bank[5][:, :nt], 1.0 / D, msq, ALU.mult, ALU.subtract)
    P.act(rstd, rstd, AF.Sqrt, bias=C.cst[:, 4:5])
    P.op("dve", lambda e: e.reciprocal(out=rstd, in_=rstd), reads=[rstd], writes=[rstd])
    P.tt("dve", msq, mean, rstd, ALU.mult)
    for c in range(8):
        x = C.xT32[:, c, tok]
        P.tt("dve", x, x, rstd, ALU.mult)
        P.tt("dve", x, x, msq, ALU.subtract)
        g = C.lng[:, l, i, c:c + 1]
        b = C.lnb[:, l, i, c:c + 1]
        P.ts("dve", x, x, g, b, ALU.mult, ALU.add)
        P.copy("act", C.xTb[:, c, tok], x)


def ffn_block(P, C, l, k):
    TC = 1024
    gT = C.ar1[:, :].rearrange("p (f t) -> p f t", f=NF)
    for tcx in range(S // TC):
        t0 = tcx * TC
        for f in range(NF):
            wb = C.win[C.win_i % 2]
            C.win_i += 1
            P.dma("pool", wb[:, :, :], C.d_ffn_in[l, k, f])
            for h in range(TC // 512):
                tok = slice(t0 + h * 512, t0 + (h + 1) * 512)
                loc = slice(h * 512, (h + 1) * 512)
                pg = C.bank[C.psg_i % 2]
                pu = C.bank[2 + C.psg_i % 2]
                C.psg_i += 1
                for c in range(8):
                    P.mm(pg[:, :], wb[:, c, 0:128], C.xTb[:, c, tok], start=(c == 0), stop=(c == 7))
                for c in range(8):
                    P.mm(pu[:, :], wb[:, c, 128:256], C.xTb[:, c, tok], start=(c == 0), stop=(c == 7))
                sg = C.sg[C.sg_i % 2]
                C.sg_i += 1
                P.act(sg[:, :], pg[:, :], AF.Silu)
                P.tt("dve", gT[:, f, loc], sg[:, :], pu[:, :], ALU.mult)
        for dc in range(8):
            wo = C.wout[C.wout_i % 2]
            C.wout_i += 1
            P.dma("pool", wo, C.d_ffn_out[l, k, dc])
            for h in range(TC // 512):
                tok = slice(t0 + h * 512, t0 + (h + 1) * 512)
                loc = slice(h * 512, (h + 1) * 512)
                py = C.bank[4 + C.psy_i % 2]
                C.psy_i += 1
                for f in range(NF):
                    P.mm(py[:, :], wo[:, f, :], gT[:, f, loc], start=(f == 0), stop=(f == NF - 1))
                xa = C.sg[C.sg_i % 2]
                C.sg_i += 1
                P.act(xa[:, :], C.xT32[:, dc, tok], AF.Copy, scale=ALPHA)
                P.stt(C.xT32[:, dc, tok], py[:, :], 0.5, xa[:, :], ALU.mult, ALU.add)
        for h in range(TC // 512):
            ln_block(P, C, l, 0 if k == 0 else 2, t0 + h * 512, 512)


def rope_tables(P, C, s):
    ang = C.f32s[:, 0:512]
    a2 = C.f32s[:, 512:1024]
    u = C.f32s[:, 1024:1536]
    r = C.f32s[:, 1536:2048]
    TWO_PI = 2 * PI

    def reduced_sin(dst, shift, post):
        P.ts("dve", a2, ang, shift, None, ALU.add)
        P.ts("dve", u, a2, 1.0 / TWO_PI, None, ALU.mult)
        P.copy("dve", C.posi[:, :], u)
        P.copy("dve", u, C.posi[:, :])
        P.stt(r, u, -TWO_PI, a2, ALU.mult, ALU.add)
        P.ts("dve", u, r, PI, -TWO_PI, ALU.is_gt, ALU.mult)
        P.tt("dve", r, r, u, ALU.add)
        P.ts("dve", u, r, -PI, TWO_PI, ALU.is_lt, ALU.mult)
        P.tt("dve", r, r, u, ALU.add)
        P.act(r, r, AF.Sin)
        P.ts("dve", dst, r, post, None, ALU.mult)

    for q in range(4):
        sl = slice(q * 512, (q + 1) * 512)
        P.dma("sp", C.posi[:, :], C.d_pos[s, :, sl])
        P.copy("dve", ang, C.posi[:, :])
        P.ts("dve", ang, ang, C.cst[:, 0:1], None, ALU.mult)
        reduced_sin(C.sinS[:, sl], 0.0, C.cst[:, 1:2])
        reduced_sin(C.cosT[:, sl], PI / 2, 1.0)


def fm_proj(P, C, l, n, dest, rope=True):
    wb = C.win[C.win_i % 2]
    C.win_i += 1
    P.dma("pool", wb[:, :, 0:128], C.d_wfm[l, n])
    for tg in range(4):
        tok = slice(tg * 512, (tg + 1) * 512)
        pq = C.bank[C.psg_i % 2]
        pr = C.bank[2 + C.psg_i % 2]
        C.psg_i += 1
        for c in range(8):
            P.mm(pq[:, :], wb[:, c, 0:128], C.xTb[:, c, tok], start=(c == 0), stop=(c == 7))
        if not rope:
            P.copy("act", dest[:, tok], pq[:, :])
            continue
        qb = C.qb16[C.sg_i % 2]
        P.copy("act", qb[:, :], pq[:, :])
        P.mm(pr[:, :], C.perm[:, :], qb[:, :])
        t1 = C.f32s[:, 2048:2560]
        t2 = C.sg[C.sg_i % 2]
        C.sg_i += 1
        P.op("dve", (lambda t2=t2, pq=pq, tok=tok: (lambda e: e.tensor_tensor(out=t2[:, :], in0=pq[:, :],
             in1=C.cosT[:, tok], op=ALU.mult)))(), reads=[pq[:, :], C.cosT[:, tok], qb[:, :]], writes=[t2[:, :]])
        P.tt("dve", t1, pr[:, :], C.sinS[:, tok], ALU.mult)
        P.tt("pool", dest[:, tok], t1, t2[:, :], ALU.add)


def tm_proj(P, C, l):
    wt = C.ar3[:, 0:8 * NTM].rearrange("p (c n) -> p c n", c=8)
    P.dma("pool", wt, C.d_wtm[l])
    P.memset("pool", C.VA[:, :, :, 64:65], 1.0)
    P.memset("pool", C.VB[:, :, :, 64:65], 1.0)
    P.memset("pool", C.VC[:, :, 64:65], 1.0)
    for tt in range(NB):
        tok = slice(tt * 128, (tt + 1) * 128)
        pa = C.bank[C.psg_i % 2]
        pb = C.bank[2 + C.psg_i % 2]
        C.psg_i += 1
        for c in range(8):
            P.mm(pa[:, :], C.xTb[:, c, tok], wt[:, c, 0:512], start=(c == 0), stop=(c == 7))
        for c in range(8):
            P.mm(pb[:, 0:200], C.xTb[:, c, tok], wt[:, c, 512:712], start=(c == 0), stop=(c == 7))
        P.copy("act", C.VA[:, tt, :, 0:64], pa[:, :].rearrange("p (h d) -> p h d", h=8))
        P.copy("dve", C.VB[:, tt, :, 0:64], pb[:, 0:128].rearrange("p (h d) -> p h d", h=2))
        P.copy("dve", C.VC[:, tt, 0:64], pb[:, 128:192])
        P.copy("dve", C.WI[:, tt, :], pb[:, 192:200])


def indexer_tile(P, C, i):
    Lk = (i + 1) * 128
    score = C.f32s[:, 0:2048]
    r = C.f32s[:, 2048:2560]
    qim = C.QIm[:, :, :, :]
    P.copy("pool", qim[0:64, 0, :, :], C.QI[0:64, :, i * 128:(i + 1) * 128])
    P.copy("pool", qim[64:128, 1, :, :], C.QI[64:128, :, i * 128:(i + 1) * 128])
    for sb0 in range(0, Lk, 512):
        w = min(512, Lk - sb0)
        for h in range(8):
            pi_ = C.bank[2 + C.psg_i % 2]
            C.psg_i += 1
            P.mm(pi_[:, 0:w], qim[:, h % 2, h // 2, :], C.KI[:, sb0:sb0 + w])
            wcol = C.WI[:, i, h:h + 1]
            if h == 0:
                P.ts("dve", score[:, sb0:sb0 + w], pi_[:, 0:w], 0.0, wcol, ALU.max, ALU.mult)
            else:
                P.act(r[:, 0:w], pi_[:, 0:w], AF.Relu)
                P.stt(score[:, sb0:sb0 + w], r[:, 0:w], wcol, score[:, sb0:sb0 + w], ALU.mult, ALU.add)
    dg = slice(i * 128, (i + 1) * 128)
    P.tt("dve", score[:, dg], score[:, dg], C.negm[:, :], ALU.add)
    lo, hi, mid, cnt, stp = (C.bis[:, k:k + 1] for k in range(5))
    W = C.bis[:, 8:8 + NITER]
    P.op("dve", lambda e: e.tensor_reduce(out=hi, in_=score[:, 0:Lk], axis=AX.X, op=ALU.max),
         reads=[score[:, 0:Lk]], writes=[hi])
    P.op("dve", lambda e: e.tensor_reduce(out=lo, in_=score[:, 0:i * 128], axis=AX.X, op=ALU.min),
         reads=[score[:, 0:i * 128]], writes=[lo])
    P.tt("dve", hi, hi, lo, ALU.subtract)
    P.ts("dve", W, C.cst[:, 8:8 + NITER], hi, None, ALU.mult)
    sel = C.sel[:, 0:Lk]
    for k in range(NITER):
        P.tt("dve", mid, lo, W[:, k:k + 1], ALU.add)
        P.ts("dve", sel, score[:, 0:Lk], mid, 0.0, ALU.is_ge, ALU.add, accum_out=cnt)
        P.ts("dve", stp, cnt, KSEL - 0.5, W[:, k:k + 1], ALU.is_ge, ALU.mult)
        P.tt("dve", lo, lo, stp, ALU.add)
    P.ts("dve", sel, score[:, 0:Lk], lo, None, ALU.is_ge)


def attention_c(P, C):
    selT = [C.bank6b[:, 0:128], C.bank7[:, 512:640]]

    class St:
        k = 0
    for i in range(NB):
        if i >= 2:
            indexer_tile(P, C, i)

        def maskf(i_, j, i=i):
            if i < 2:
                return C.cmask[:, 5, :] if j == i else None
            buf = selT[St.k % 2]
            St.k += 1
            P.transpose(buf, C.sel[:, j * 128:(j + 1) * 128], C.ident[:, :])
            return buf
        attention_one(P, C, i, maskf)


def attention_one(P, C, i, maskf):
    attention_tiles(P, C, [i], None,
                    lambda h: C.KC,
                    lambda h, j: C.VC[:, j, 0:65], lambda i_: list(range(i_ + 1)), maskf, 2)


def attention_tiles(P, C, tiles, qview, kview, vview, jlist, maskf, br, sinkexp=None):
    DLY = 2
    for i in tiles:
        qs = slice(i * 128, (i + 1) * 128)
        js = jlist(i)
        po = [C.bank[4], C.bank[5]]
        pend = []
        qm = C.Qm[C.qm_i % 2]
        C.qm_i += 1
        P.copy("pool", qm[0:64, 0, :, :], C.Q[0:64, :, qs])
        P.copy("pool", qm[64:128, 1, :, :], C.Q[64:128, :, qs])

        def emit_pv(u):
            (jn_, j_, par_, ev_) = u
            for hh in range(4):
                h = 2 * hh + par_
                P.mm(po[par_][:, hh * 65:(hh + 1) * 65], ev_[:, hh, :], vview(h, j_),
                     start=(jn_ == 0 and hh == 0), stop=(jn_ == len(js) - 1 and hh == 3))

        for (jn, j, par) in [(jn, j, par) for par in range(2) for jn, j in enumerate(js)]:
            ks = slice(j * 128, (j + 1) * 128)
            m = maskf(i, j)
            kk = C.pss_i % 2
            C.pss_i += 1
            if True:
                pss = C.bank[2 * kk + par]
                pv = pss[:, :].rearrange("p (h q) -> p h q", h=4)
                for hh in range(4):
                    h = 2 * hh + par
                    P.mm(pv[:, hh, :], kview(h)[:, ks], qm[:, par, h // 2, :])
                e = C.ebuf[C.e_i % 4]
                C.e_i += 1
                P.act(e[:, :], pss[:, :], AF.Exp, scale=0.125)
                ev = e[:, :].rearrange("p (h q) -> p h q", h=4)
                if m is not None:
                    mb = m.unsqueeze(1).broadcast_to([128, 4, 128])
                    P.tt("dve", ev, ev, mb, ALU.mult)
                pend.append((jn, j, par, ev))
                if len(pend) > DLY:
                    emit_pv(pend.pop(0))
        while pend:
            emit_pv(pend.pop(0))
        den = C.den[:, :]
        for par in range(2):
            dv = po[par][:, 0:260].rearrange("p (h d) -> p h d", h=4)[:, :, 64]
            if sinkexp is not None:
                sk = sinkexp[:, :].rearrange("p (hh g) -> p g hh", g=2)[:, par, :]
                P.tt("dve", den[:, par * 4:(par + 1) * 4], dv, sk, ALU.add)
            else:
                P.copy("dve", den[:, par * 4:(par + 1) * 4], dv)
        P.op("dve", lambda e_: e_.reciprocal(out=den, in_=den), reads=[den], writes=[den])
        on = C.on[:, :, :]
        for h in range(8):
            par, hh = h % 2, h // 2
            P.act(on[:, h, :], po[par][:, hh * 65:hh * 65 + 64], AF.Copy,
                  scale=den[:, par * 4 + hh:par * 4 + hh + 1])
        ptr = C.bank7[:, 0:512].rearrange("p (c q) -> p c q", c=4)
        for c in range(4):
            P.transpose(ptr[:, c, :], on[:, 2 * c:2 * c + 2, :].rearrange("p h d -> p (h d)"), C.ident[:, :])
        ot = C.otst[0]
        P.copy("dve", ot[:, :, :], ptr)
        P.dma("sp", C.d_oT[br, :, :, qs], ot[:, :, :])


def merge_block(P, C, l):
    oTc = C.ar1[:, 0:6144].rearrange("p (x c t) -> p x c t", x=3, c=4)
    mT = C.ar1[:, 6144:10240].rearrange("p (m t) -> p m t", m=8)
    wo = C.ar1[:, 10240:12288].rearrange("p (i m j) -> p i m j", i=2, m=8)
    mg = C.f32s[:, 1024:1536]
    tmp = C.f32s[:, 1536:2048]
    for tg in range(4):
        tok = slice(tg * 512, (tg + 1) * 512)
        for x in range(3):
            P.dma("sp", oTc[:, x, :, :], C.d_oT[x, :, :, tok])
        for mc in range(8):
            for x in range(3):
                wb = C.win[C.win_i % 2]
                C.win_i += 1
                P.dma("pool", wb[:, :, 0:128], C.d_wfm[l, 24 + x * 8 + mc])
                P.dma("pool", wb[:, 0:4, 128:256], C.d_wbr[l, x, mc])
                pg = C.bank[C.psg_i % 2]
                pb = C.bank[2 + C.psg_i % 2]
                C.psg_i += 1
                for c in range(8):
                    P.mm(pg[:, :], wb[:, c, 0:128], C.xTb[:, c, tok], start=(c == 0), stop=(c == 7))
                for cc in range(4):
                    P.mm(pb[:, :], wb[:, cc, 128:256], oTc[:, x, cc, :], start=(cc == 0), stop=(cc == 3))
                sg = C.sg[C.sg_i % 2]
                C.sg_i += 1
                P.act(sg[:, :], pg[:, :], AF.Sigmoid)
                if x == 0:
                    P.tt("dve", mg, sg[:, :], pb[:, :], ALU.mult)
                else:
                    P.tt("dve", tmp, sg[:, :], pb[:, :], ALU.mult)
                    P.tt("pool", mg, mg, tmp, ALU.add)
            P.copy("act", mT[:, mc, :], mg)
        for dc in range(8):
            w = wo[:, C.wo_i % 2, :, :]
            C.wo_i += 1
            P.dma("pool", w, C.d_wo[l, dc])
            py = C.bank[4 + C.psy_i % 2]
            C.psy_i += 1
            for mc in range(8):
                P.mm(py[:, :], w[:, mc, :], mT[:, mc, :], start=(mc == 0), stop=(mc == 7))
            xa = C.sg[C.sg_i % 2]
            C.sg_i += 1
            P.act(xa[:, :], C.xT32[:, dc, tok], AF.Copy, scale=ALPHA)
            P.stt(C.xT32[:, dc, tok], py[:, :], 1.0, xa[:, :], ALU.mult, ALU.add)
        ln_block(P, C, l, 1, tg * 512, 512)


def mixer_block(P, C, l, s, dbg=None):
    import os
    stop = os.environ.get("MIXSTOP", "")
    tm_proj(P, C, l)
    rope_tables(P, C, s)
    if stop == "tm":
        return
    hb = lambda h: slice((h % 2) * 64, (h % 2) * 64 + 64)
    for n in range(4):
        fm_proj(P, C, l, n, C.Q[:, n, :])
        fm_proj(P, C, l, 4 + n, C.K[:, n, :])

    def maskA(i, j):
        o = i - j
        return C.cmask[:, {0: 0, 1: 1, 2: 2, 3: 2, 4: 3}.get(o, 4), :]
    attention_tiles(P, C, range(NB), None, lambda h: C.K[:, h // 2, :],
                    lambda h, j: C.VA[:, j, h, 0:65], lambda i: list(range(i + 1)), maskA, 0)
    if stop == "A":
        return
    for n in range(4):
        fm_proj(P, C, l, 8 + n, C.Q[:, n, :])
    fm_proj(P, C, l, 12, C.K[:, 0, :])
    fm_proj(P, C, l, 13, C.K[:, 1, :])
    P.dma("sp", C.sinkexp[:, :], C.d_sink[l])
    P.act(C.sinkexp[:, :], C.sinkexp[:, :], AF.Exp)

    def kB(h):
        gk = h // 4
        return C.K[:, 0 if gk == (h % 2) else 1, :]
    attention_tiles(P, C, range(NB), None, kB,
                    lambda h, j: C.VB[:, j, h // 4, 0:65], lambda i: ([i - 1, i] if i > 0 else [0]),
                    lambda i, j: C.cmask[:, 5 if j == i else 6, :], 1, sinkexp=C.sinkexp)
    if stop == "B":
        return
    for n in range(4):
        fm_proj(P, C, l, 14 + n, C.Q[:, n, :])
    fm_proj(P, C, l, 18, C.K[:, 0, :])
    for n in range(4):
        fm_proj(P, C, l, 19 + n, C.QI[:, n, :])
    fm_proj(P, C, l, 23, C.K[:, 1, :])
    C.KC = C.K[:, 0, :]
    C.KI = C.K[:, 1, :]
    attention_c(P, C)
    if stop == "C":
        return
    merge_block(P, C, l)


def build_program(n_layers=L, stages=("ffn1", "mixer", "ffn2"), n_seq=SEQ_PER_CORE, dbg=False):
    nc = bass.Bass("TRN2", target_bir_lowering=False)
    C = Ctx()

    def din(name, shape, dt=F32):
        return nc.dram_tensor(name, shape, dt, kind="ExternalInput").ap()
    C.d_xT = din("xT", [SEQ_PER_CORE, 128, 8, S])
    C.d_pos = din("posb", [SEQ_PER_CORE, 128, S], I32)
    C.d_ffn_in = din("ffn_in", [L, 2, NF, 128, 8, 256])
    C.d_ffn_out = din("ffn_out", [L, 2, 8, 128, NF, 128])
    C.d_lng = din("lng", [128, L, 3, 8])
    C.d_lnb = din("lnb", [128, L, 3, 8])
    C.d_wfm = din("wfm", [L, NFM, 128, 8, 128])
    C.d_wtm = din("wtm", [L, 128, 8, NTM])
    C.d_wbr = din("wbr", [L, 3, 8, 128, 4, 128])
    C.d_wo = din("wo", [L, 8, 128, 8, 128])
    C.d_sink = din("sink", [L, 128, 8])
    C.d_cst = din("cst", [128, 32])
    C.d_cmask = din("cmask", [128, 7, 128])
    C.d_negm = din("negm", [128, 128])
    C.d_pi = din("permident", [128, 2, 128])
    C.d_out = nc.dram_tensor("outT", [SEQ_PER_CORE, 128, 8, S], F32, kind="ExternalOutput").ap()
    C.d_oT = nc.dram_tensor("oT_scr", [3, 128, 4, S], BF16, kind=("ExternalOutput" if dbg else "Internal")).ap()
    P = Prog(nc)
    for n in ("xT", "posb", "ffn_in", "ffn_out", "lng", "lnb", "wfm", "wtm", "wbr", "wo", "sink", "cst",
              "cmask", "negm", "permident"):
        P.mark_readonly(n)
    with contextlib.ExitStack() as st:
        def sb(name, shape, dt):
            return st.enter_context(nc.sbuf_tensor(name, shape, dt))

        C.xT32 = sb("xT32", [128, 8, S], F32)
        C.xTb = sb("xTb", [128, 8, S], BF16)
        C.ar1 = sb("ar1", [128, NF * 1024], BF16)
        C.ar2 = sb("ar2", [128, 8448], BF16)
        C.ar3 = sb("ar3", [128, 6144], BF16)
        C.f32s = sb("f32s", [128, 2560], F32)
        C.win = [sb("win%d" % i, [128, 8, 256], BF16) for i in range(2)]
        C.lng = sb("lng_s", [128, L, 3, 8], F32)
        C.lnb = sb("lnb_s", [128, L, 3, 8], F32)
        C.ones_b = sb("ones_b", [128, 128], BF16)
        C.cst = sb("cst_s", [128, 32], F32)
        C.cmask = sb("cmask_s", [128, 7, 128], BF16)
        C.negm = sb("negm_s", [128, 128], F32)
        C.pi = sb("pi_s", [128, 2, 128], BF16)
        C.qb16 = [sb("qb16_0", [128, 512], BF16)] * 2
        C.ebuf = [sb("ebuf%d" % i, [128, 512], BF16) for i in range(4)]
        C.on = sb("on", [128, 8, 64], BF16)
        C.Qm = [sb("Qm%d" % i, [128, 2, 4, 128], BF16) for i in range(2)]
        C.QIm = sb("QIm", [128, 2, 4, 128], BF16)
        C.otst = [sb("otst0", [128, 4, 128], BF16)]
        C.WI = sb("WI", [128, NB, 8], F32)
        C.den = sb("den", [128, 8], F32)
        C.sinkexp = sb("sinkexp", [128, 8], F32)
        C.bis = sb("bis", [128, 32], F32)
        C.bank = [st.enter_context(nc.psum_tensor("bank%d" % i, [128, 512], F32)) for i in range(6)]
        C.bank6b = st.enter_context(nc.psum_tensor("bank6b", [128, 1024], BF16))
        C.bank7 = st.enter_context(nc.psum_tensor("bank7", [128, 1024], BF16))
        C.cosT = C.ar3[:, 0:2048]
        C.sinS = C.ar3[:, 2048:4096]
        C.posi = C.f32s[:, 2048:2560].bitcast(I32)
        C.perm = C.pi[:, 0, :]
        C.ident = C.pi[:, 1, :]
        C.sg = [C.f32s[:, 0:512], C.f32s[:, 512:1024]]
        C.st_a = C.f32s[:, 1024:1536]
        C.st_b = C.f32s[:, 1536:2048]
        C.st_c = C.f32s[:, 2048:2560]
        C.zb = C.ar2[:, 0:4096].rearrange("p (c t) -> p c t", c=8)
        C.zq = C.ar2[:, 4096:8192].rearrange("p (c t) -> p c t", c=8)
        C.wout = [C.ar3[:, i * 2816:(i + 1) * 2816].rearrange("p (f j) -> p f j", f=NF) for i in range(2)]
        C.Q = C.ar1[:, 0:8192].rearrange("p (c t) -> p c t", c=4)
        C.K = C.ar1[:, 8192:16384].rearrange("p (c t) -> p c t", c=4)
        C.VB = C.ar1[:, 16384:18496].rearrange("p (t h d) -> p t h d", t=NB, h=2)
        C.VC = C.ar1[:, 18496:19552].rearrange("p (t d) -> p t d", t=NB)
        C.sel = C.ar1[:, 19552:21600]
        C.VA = C.ar2[:, 0:8448].rearrange("p (t h d) -> p t h d", t=NB, h=8)
        C.QI = C.ar2[:, 0:8192].rearrange("p (c t) -> p c t", c=4)
        C.win_i = C.wout_i = C.psg_i = C.psy_i = C.sg_i = C.pss_i = C.e_i = C.ot_i = C.wo_i = 0

        P.dma("sp", C.lng[:, :, :, :], C.d_lng)
        P.dma("sp", C.lnb[:, :, :, :], C.d_lnb)
        P.dma("sp", C.cst[:, :], C.d_cst)
        P.dma("sp", C.negm[:, :], C.d_negm)
        P.dma("pool", C.cmask[:, :, :], C.d_cmask)
        P.dma("pool", C.pi[:, :, :], C.d_pi)
        P.memset("dve", C.ones_b[:, :], 1.0)
        for qmz in C.Qm + [C.QIm]:
            P.memset("pool", qmz[:, :, :, :], 0.0)
        C.qm_i = 0
        finals = []
        for s in range(n_seq):
            for c in range(8):
                P.dma("sp", C.xT32[:, c, :], C.d_xT[s, :, c, :])
            for c in range(8):
                P.copy("act", C.xTb[:, c, :], C.xT32[:, c, :])
            for l in range(n_layers):
                if "ffn1" in stages:
                    ffn_block(P, C, l, 0)
                if "mixer" in stages:
                    mixer_block(P, C, l, s)
                if "ffn2" in stages:
                    ffn_block(P, C, l, 1)
            for c in range(8):
                finals.append(P.dma("sp", C.d_out[s, :, c, :], C.xT32[:, c, :]))
        P.finalize(finals)
    C.P = P
    return nc, C


_OFF = {}
_o = 0
for _n, _w in (("qa", 512), ("ka", 512), ("va", 512), ("qb", 512), ("kb", 128), ("vb", 128), ("qc", 512),
               ("kc", 64), ("vc", 64), ("qi", 512), ("ki", 64), ("wi", 8), ("ga", 1024), ("gb", 1024),
               ("gc", 1024)):
    _OFF[_n] = _o
    _o += _w


def _fm_cols():
    cols = []
    r = lambda a, n: list(range(a, a + n))
    for n in range(4):
        cols.append(r(_OFF["qa"] + 128 * n, 128))
    for n in range(4):
        cols.append(r(_OFF["ka"] + 128 * n, 128))
    for n in range(4):
        cols.append(r(_OFF["qb"] + 128 * n, 128))
    cols.append(r(_OFF["kb"], 128))
    cols.append(r(_OFF["kb"] + 64, 64) + r(_OFF["kb"], 64))
    for n in range(4):
        cols.append(r(_OFF["qc"] + 128 * n, 128))
    cols.append(r(_OFF["kc"], 64) * 2)
    for n in range(4):
        cols.append(r(_OFF["qi"] + 128 * n, 128))
    cols.append(r(_OFF["ki"], 64) * 2)
    for g in ("ga", "gb", "gc"):
        for n in range(8):
            cols.append(r(_OFF[g] + 128 * n, 128))
    return np.array(cols)


def host_consts():
    f32 = np.float32
    p = np.arange(128)
    cst = np.zeros((128, 32), f32)
    cst[:, 0] = (10000.0 ** (-(2.0 * (p % 32)) / 64.0)).astype(f32)
    cst[:, 1] = np.where((p % 64) < 32, -1.0, 1.0)
    cst[:, 2] = -np.pi
    cst[:, 4] = EPS
    cst[:, 8:8 + NITER] = (0.5 ** (np.arange(NITER) + 1))[None, :]
    k = p[:, None]
    q = p[None, :]

    def mult(delta):
        m = ((delta >= 0) & (delta <= 128)).astype(f32)
        m += ((delta >= 0) & (delta <= 512) & (delta % 4 == 0))
        m += ((delta >= 0) & (delta <= 2048) & (delta % 16 == 0))
        return m
    cm = np.zeros((128, 7, 128), f32)
    for idx, o in enumerate((0, 1, 2, 4, 5)):
        cm[:, idx, :] = mult(128 * o + q - k)
    cm[:, 5, :] = (k <= q)
    cm[:, 6, :] = (k > q)
    negm = np.where(p[None, :] <= p[:, None], 0.0, -1e30).astype(f32)
    perm = np.zeros((128, 128), f32)
    partner = np.where((p % 64) < 32, p + 32, p - 32)
    perm[partner, p] = 1.0
    pi = np.stack([perm, np.eye(128, dtype=f32)], 1)
    return {"cst": cst, "cmask": cm, "negm": negm, "permident": np.ascontiguousarray(pi)}


def host_layout(inputs):
    f32 = np.float32
    x = np.asarray(inputs["x"], f32)
    B = x.shape[0]
    xT = np.ascontiguousarray(x.reshape(B, S, 8, 128).transpose(0, 3, 2, 1))
    pos = np.asarray(inputs["positions"]).astype(np.int32)
    posb = np.ascontiguousarray(np.broadcast_to(pos[:, None, :], (B, 128, S)))
    shared = host_consts()
    fin = np.stack([np.asarray(inputs["ffn1_in"], f32), np.asarray(inputs["ffn2_in"], f32)], 1)
    g = fin[..., :DFF].reshape(L, 2, 8, 128, NF, 128)
    u = fin[..., DFF:].reshape(L, 2, 8, 128, NF, 128)
    gu = np.concatenate([g, u], axis=-1)
    shared["ffn_in"] = np.ascontiguousarray(gu.transpose(0, 1, 4, 3, 2, 5))
    fo = np.stack([np.asarray(inputs["ffn1_out"], f32), np.asarray(inputs["ffn2_out"], f32)], 1)
    fo = fo.reshape(L, 2, NF, 128, 8, 128)
    shared["ffn_out"] = np.ascontiguousarray(fo.transpose(0, 1, 4, 3, 2, 5))
    shared["lng"] = np.ascontiguousarray(np.asarray(inputs["ln_g"], f32).reshape(L, 3, 8, 128).transpose(3, 0, 1, 2))
    shared["lnb"] = np.ascontiguousarray(np.asarray(inputs["ln_b"], f32).reshape(L, 3, 8, 128).transpose(3, 0, 1, 2))
    w_in = np.asarray(inputs["w_in"], f32)
    fm = w_in[:, :, _fm_cols()]
    fm = fm.reshape(L, 8, 128, NFM, 128)
    shared["wfm"] = np.ascontiguousarray(fm.transpose(0, 3, 2, 1, 4))
    tmc = np.concatenate([np.arange(_OFF["va"], _OFF["va"] + 512), np.arange(_OFF["vb"], _OFF["vb"] + 128),
                          np.arange(_OFF["vc"], _OFF["vc"] + 64), np.arange(_OFF["wi"], _OFF["wi"] + 8)])
    tm = np.zeros((L, D, NTM), f32)
    tm[:, :, :712] = w_in[:, :, tmc]
    tm = tm.reshape(L, 8, 128, NTM)
    shared["wtm"] = np.ascontiguousarray(tm.transpose(0, 2, 1, 3))
    wbr = np.stack([np.asarray(inputs[k], f32) for k in ("w_br_a", "w_br_b", "w_br_c")], 1)
    wbr = wbr.reshape(L, 3, 4, 128, 8, 128)
    shared["wbr"] = np.ascontiguousarray(wbr.transpose(0, 1, 4, 3, 2, 5))
    wo = np.asarray(inputs["w_out"], f32).reshape(L, 8, 128, 8, 128)
    shared["wo"] = np.ascontiguousarray(wo.transpose(0, 3, 2, 1, 4))
    sink = np.asarray(inputs["sink_b"], f32)
    shared["sink"] = np.ascontiguousarray(np.broadcast_to(sink[:, None, :], (L, 128, 8)))
    in_maps = []
    for c in range(NCORES):
        m = dict(shared)
        m["xT"] = np.ascontiguousarray(xT[c * SEQ_PER_CORE:(c + 1) * SEQ_PER_CORE])
        m["posb"] = np.ascontiguousarray(posb[c * SEQ_PER_CORE:(c + 1) * SEQ_PER_CORE])
        in_maps.append(m)
    return in_maps


def gather(results):
    outs = [r["outT"] for r in results]
    o = np.concatenate(outs, 0)
    B = o.shape[0]
    return np.ascontiguousarray(o.transpose(0, 3, 2, 1).reshape(B, S, D)).astype(np.float32)


def kernel(**inputs):
    in_maps = host_layout(inputs)
    nc, _ = build_program()
    res = run_bass_kernel_spmd(nc, in_maps, core_ids=list(range(NCORES)))
    return gather(res.results)
```
